# Optimizing a Trainium2 kernel written in Bass

```python
import jax
import jax.numpy as jnp
from jax import lax
import numpy as np

D_MODEL = 1024
BATCH = 2
SEQ = 8192
DEPTH = 2
DEC_BATCH = 128
DEC_SEQ = 8
PAST_LEN = 8192
PAGE_SIZE = 128

N_MIXERS = 2
N_A_LAYERS = (DEPTH + 1) // 2
N_B_LAYERS = DEPTH // 2
A_GROUPS = ((128, 1), (512, 4), (2048, 16))
A_N_GROUPS = len(A_GROUPS)
A_HEADS = 8
A_HEAD_DIM = D_MODEL // A_HEADS
A_WIDTH = A_HEADS * A_HEAD_DIM
B_WINDOW = 128
B_Q_HEADS = 16
B_KV_HEADS = 4
B_GROUP = B_Q_HEADS // B_KV_HEADS
B_HEAD_DIM = 64
B_Q_WIDTH = B_Q_HEADS * B_HEAD_DIM
B_KV_WIDTH = 2 * B_KV_HEADS * B_HEAD_DIM
BLK = 128
D_FF = -(-8 * D_MODEL // (3 * 256)) * 256
RMS_EPS = 1e-6
NEG_INF = -1e30

kernel_name = "hybrid_dilated_swa_sink_adaln_decoder_step"


def rms_norm(x, g):
    xf = x.astype(jnp.float32)
    y = xf * lax.rsqrt(jnp.mean(xf * xf, axis=-1, keepdims=True) + RMS_EPS)
    return (y * g.astype(jnp.float32)).astype(x.dtype)


def alibi_slopes(n):
    return 2.0 ** (-8.0 * jnp.arange(1, n + 1, dtype=jnp.float32) / n)


def a_slopes():
    return alibi_slopes(A_N_GROUPS * A_HEADS).reshape(A_N_GROUPS, A_HEADS)


def b_slopes():
    return alibi_slopes(B_Q_HEADS).reshape(B_KV_HEADS, B_GROUP)


def swiglu(h, w_gate, w_up, w_down):
    return (jax.nn.silu(h @ w_gate) * (h @ w_up)) @ w_down


def softmax_stats(s, sink):
    m = s.max(-1)
    if sink is not None:
        m = jnp.maximum(m, sink)
    p = jnp.exp(s - m[..., None])
    l = p.sum(-1)
    if sink is not None:
        l = l + jnp.exp(sink - m)
    return p, l, m + jnp.log(l)


def banded_attention(q, k, v, slopes, step, window, sinks):
    n, s_len, hk, g, dh = q.shape
    nb = s_len // BLK
    qb = q.reshape(n, nb, BLK, hk, g, dh)

    def band(t):
        tb = t.reshape(n, nb, BLK, hk, dh)
        prev = jnp.concatenate([jnp.zeros_like(tb[:, :1]), tb[:, :-1]], axis=1)
        return jnp.concatenate([prev, tb], axis=2)

    kb, vb = band(k), band(v)
    s = jnp.einsum('bcqhgd,bckhd->bchgqk', qb, kb, preferred_element_type=jnp.float32) * (dh ** -0.5)
    dist = (jnp.arange(BLK)[:, None] + BLK) - jnp.arange(2 * BLK)[None, :]
    key_row = jnp.arange(nb)[:, None] * BLK - BLK + jnp.arange(2 * BLK)[None, :]
    valid = ((dist >= 0) & (dist <= window))[None] & (key_row >= 0)[:, None, :]
    bias = -(slopes.astype(jnp.float32) * step)[:, :, None, None] * dist.astype(jnp.float32)
    s = jnp.where(valid[None, :, None, None], s + bias, NEG_INF)
    sink = None if sinks is None else sinks.astype(jnp.float32)[:, :, None]
    p, l, lse = softmax_stats(s, sink)
    o = jnp.einsum('bchgqk,bckhd->bcqhgd', p, vb, preferred_element_type=jnp.float32)
    o = o / jnp.moveaxis(l, -1, 2)[..., None]
    return o.reshape(n, s_len, hk, g, dh), jnp.moveaxis(lse, -1, 2).reshape(n, s_len, hk, g)


def gathered_attention(q, kv_cat, slopes, step, window, sinks):
    n, t, hk, g, dh = q.shape
    past = kv_cat.shape[1] - t
    j = jnp.arange(window // step + 1)
    idx = past + jnp.arange(t)[:, None] - j[None, :] * step
    kvg = jnp.take(kv_cat, jnp.maximum(idx, 0), axis=1)
    s = jnp.einsum('bthgd,btjhd->bthgj', q, kvg[:, :, :, 0], preferred_element_type=jnp.float32) * (dh ** -0.5)
    bias = -(slopes.astype(jnp.float32) * step)[:, :, None] * j.astype(jnp.float32)
    s = jnp.where((idx >= 0)[None, :, None, None, :], s + bias, NEG_INF)
    sink = None if sinks is None else sinks.astype(jnp.float32)
    p, l, lse = softmax_stats(s, sink)
    o = jnp.einsum('bthgj,btjhd->bthgd', p, kvg[:, :, :, 1], preferred_element_type=jnp.float32)
    return o / l[..., None], lse


def dilated_prompt(q, k, v, slopes, window, dil):
    b, s_len, hh, dh = q.shape
    n_sub = s_len // dil
    n_pad = -(-n_sub // BLK) * BLK

    def to_sub(t):
        t = t.reshape(b, n_sub, dil, hh, dh).transpose(0, 2, 1, 3, 4).reshape(b * dil, n_sub, hh, dh)
        return jnp.pad(t, ((0, 0), (0, n_pad - n_sub), (0, 0), (0, 0)))

    o, lse = banded_attention(to_sub(q)[:, :, :, None], to_sub(k), to_sub(v), slopes[:, None], dil, window // dil, None)
    o = o[:, :n_sub, :, 0].reshape(b, dil, n_sub, hh, dh).transpose(0, 2, 1, 3, 4).reshape(b, s_len, hh, dh)
    lse = lse[:, :n_sub, :, 0].reshape(b, dil, n_sub, hh).transpose(0, 2, 1, 3).reshape(b, s_len, hh)
    return o, lse


def combine_by_denominator(outs, lses):
    w = jax.nn.softmax(jnp.stack(lses), axis=0)
    return jnp.einsum('gbsh,gbshd->bshd', w, jnp.stack(outs))


def mixer_a_prompt(h, w_in, w_out):
    b, s_len, _ = h.shape
    qkv = (h @ w_in).reshape(b, s_len, A_N_GROUPS, 3, A_HEADS, A_HEAD_DIM)
    slopes = a_slopes()
    outs, lses, rows = [], [], []
    for g, (window, dil) in enumerate(A_GROUPS):
        o, lse = dilated_prompt(qkv[:, :, g, 0], qkv[:, :, g, 1], qkv[:, :, g, 2], slopes[g], window, dil)
        outs.append(o)
        lses.append(lse)
        keep = min(window, s_len)
        rows.append(qkv[:, s_len - keep:, g, 1:3])
    o = combine_by_denominator(outs, lses).reshape(b, s_len, A_WIDTH).astype(h.dtype)
    return o @ w_out, rows


def mixer_a_sample(h, caches, w_in, w_out):
    b, t, _ = h.shape
    qkv = (h @ w_in).reshape(b, t, A_N_GROUPS, 3, A_HEADS, A_HEAD_DIM)
    slopes = a_slopes()
    outs, lses, rows = [], [], []
    for g, (window, dil) in enumerate(A_GROUPS):
        kv_new = qkv[:, :, g, 1:3]
        kv_cat = jnp.concatenate([caches[g], kv_new.astype(caches[g].dtype)], axis=1)
        o, lse = gathered_attention(qkv[:, :, g, 0][:, :, :, None], kv_cat, slopes[g][:, None], dil, window, None)
        outs.append(o[:, :, :, 0])
        lses.append(lse[:, :, :, 0])
        rows.append(kv_new)
    o = combine_by_denominator(outs, lses).reshape(b, t, A_WIDTH).astype(h.dtype)
    return o @ w_out, rows


def mixer_b_prompt(h, w_in, sinks, w_out):
    b, s_len, _ = h.shape
    proj = h @ w_in
    q = proj[..., :B_Q_WIDTH].reshape(b, s_len, B_KV_HEADS, B_GROUP, B_HEAD_DIM)
    kv = proj[..., B_Q_WIDTH:].reshape(b, s_len, 2, B_KV_HEADS, B_HEAD_DIM)
    o, _ = banded_attention(q, kv[:, :, 0], kv[:, :, 1], b_slopes(), 1, B_WINDOW, sinks)
    keep = min(B_WINDOW, s_len)
    return o.reshape(b, s_len, B_Q_WIDTH).astype(h.dtype) @ w_out, kv[:, s_len - keep:]


def mixer_b_sample(h, cache, w_in, sinks, w_out):
    b, t, _ = h.shape
    proj = h @ w_in
    q = proj[..., :B_Q_WIDTH].reshape(b, t, B_KV_HEADS, B_GROUP, B_HEAD_DIM)
    kv_new = proj[..., B_Q_WIDTH:].reshape(b, t, 2, B_KV_HEADS, B_HEAD_DIM)
    kv_cat = jnp.concatenate([cache, kv_new.astype(cache.dtype)], axis=1)
    o, _ = gathered_attention(q, kv_cat, b_slopes(), 1, B_WINDOW, sinks)
    return o.reshape(b, t, B_Q_WIDTH).astype(h.dtype) @ w_out, kv_new


def setup_inputs(seed: int = 0) -> dict:
    key = jax.random.key(seed)
    ks = jax.random.split(key, 24)

    def nrm(k, shape, scale=1.0):
        return jax.random.normal(k, shape, jnp.float32) * scale

    d = D_MODEL
    inp = {}
    inp['x_prompt'] = nrm(ks[0], (BATCH, SEQ, d))
    inp['x_sample'] = nrm(ks[1], (DEC_BATCH, DEC_SEQ, d))
    for gi, (window, _) in enumerate(A_GROUPS):
        inp['cache_a_w%d' % window] = nrm(ks[2 + gi], (N_A_LAYERS, DEC_BATCH, min(window, PAST_LEN), 2, A_HEADS, A_HEAD_DIM))
    inp['cache_b'] = nrm(ks[5], (N_B_LAYERS, DEC_BATCH, min(B_WINDOW, PAST_LEN), 2, B_KV_HEADS, B_HEAD_DIM))
    inp['c_prompt'] = nrm(ks[6], (BATCH, d))
    inp['c_sample'] = nrm(ks[7], (DEC_BATCH, d))
    inp['norm_mix_g'] = 1.0 + nrm(ks[8], (DEPTH, d), 0.1)
    inp['norm_ffn_g'] = 1.0 + nrm(ks[9], (DEPTH, d), 0.1)
    inp['w_ada'] = nrm(ks[10], (DEPTH, d, 6 * d), 0.5 * d ** -0.5)
    inp['b_ada'] = nrm(ks[11], (DEPTH, 6 * d), 0.02)
    inp['w_a_in'] = nrm(ks[12], (N_A_LAYERS, d, A_N_GROUPS * 3 * A_WIDTH), d ** -0.5)
    inp['w_a_out'] = nrm(ks[13], (N_A_LAYERS, A_WIDTH, d), A_WIDTH ** -0.5)
    inp['w_b_in'] = nrm(ks[14], (N_B_LAYERS, d, B_Q_WIDTH + B_KV_WIDTH), d ** -0.5)
    inp['b_sinks'] = nrm(ks[15], (N_B_LAYERS, B_KV_HEADS, B_GROUP), 0.5)
    inp['w_b_out'] = nrm(ks[16], (N_B_LAYERS, B_Q_WIDTH, d), B_Q_WIDTH ** -0.5)
    inp['w_ffn_gate'] = nrm(ks[17], (DEPTH, d, D_FF), d ** -0.5)
    inp['w_ffn_up'] = nrm(ks[18], (DEPTH, d, D_FF), d ** -0.5)
    inp['w_ffn_down'] = nrm(ks[19], (DEPTH, D_FF, d), D_FF ** -0.5)
    inp['norm_final_g'] = 1.0 + nrm(ks[20], (d,), 0.1)
    return inp


def reference(x_prompt, x_sample, cache_a_w128, cache_a_w512, cache_a_w2048, cache_b, c_prompt, c_sample,
              norm_mix_g, norm_ffn_g, w_ada, b_ada, w_a_in, w_a_out, w_b_in, b_sinks, w_b_out,
              w_ffn_gate, w_ffn_up, w_ffn_down, norm_final_g):
    a_caches = (cache_a_w128, cache_a_w512, cache_a_w2048)

    def trunk(x, c, sample):
        a_rows = [[] for _ in A_GROUPS]
        b_rows = []
        for i in range(DEPTH):
            li = i // N_MIXERS
            mod = (jax.nn.silu(c) @ w_ada[i] + b_ada[i])[:, None, :]
            shift1, scale1, gate1, shift2, scale2, gate2 = jnp.split(mod, 6, axis=-1)
            h = rms_norm(x, norm_mix_g[i]) * (1 + scale1) + shift1
            if i % N_MIXERS == 0:
                if sample:
                    y, rows = mixer_a_sample(h, [cc[li] for cc in a_caches], w_a_in[li], w_a_out[li])
                else:
                    y, rows = mixer_a_prompt(h, w_a_in[li], w_a_out[li])
                for g in range(A_N_GROUPS):
                    a_rows[g].append(rows[g])
            else:
                if sample:
                    y, rows = mixer_b_sample(h, cache_b[li], w_b_in[li], b_sinks[li], w_b_out[li])
                else:
                    y, rows = mixer_b_prompt(h, w_b_in[li], b_sinks[li], w_b_out[li])
                b_rows.append(rows)
            x = x + gate1 * y
            h = rms_norm(x, norm_ffn_g[i]) * (1 + scale2) + shift2
            x = x + gate2 * swiglu(h, w_ffn_gate[i], w_ffn_up[i], w_ffn_down[i])
        return rms_norm(x, norm_final_g), [jnp.stack(r) for r in a_rows], jnp.stack(b_rows)

    y_prompt, (p_a128, p_a512, p_a2048), p_b = trunk(x_prompt, c_prompt, False)
    y_sample, (s_a128, s_a512, s_a2048), s_b = trunk(x_sample, c_sample, True)
    return (y_prompt, y_sample, p_a128, p_a512, p_a2048, p_b, s_a128, s_a512, s_a2048, s_b)
```

```python
import contextlib
import os
import numpy as np
import concourse.bass as bass
import concourse.mybir as mybir
from concourse.bass_utils import run_bass_kernel_spmd

F32 = mybir.dt.float32
BF16 = mybir.dt.bfloat16
ALU = mybir.AluOpType
AF = mybir.ActivationFunctionType

SAME_ENGINE_SYNC = True
ENGS = ("pe", "act", "dve", "pool", "sp")
NEG = -30000.0

D = 1024
DFF = 2816
NCT = DFF // 128
NEXT = 4224
NOWN = 2176
NTOK = NOWN + 128
A_GROUPS = ((128, 1), (512, 4), (2048, 16))
FFN_GROUPS = 4
LOOK = 2


class Res:
    __slots__ = ("name", "writers", "readers", "loose")

    def __init__(self, name, init=(), loose=False):
        self.name = name
        self.writers = list(init)
        self.readers = []
        self.loose = loose


class Chan:
    __slots__ = ("sem", "count", "name", "last")

    def __init__(self, name):
        self.name = name
        self.sem = None
        self.count = 0
        self.last = None


class Op:
    __slots__ = ("eng", "fn", "kind", "deps", "chan", "chan_count", "sig", "sig_count", "seq")

    def __init__(self, eng, fn, kind):
        self.eng = eng
        self.fn = fn
        self.kind = kind
        self.deps = []
        self.chan = None
        self.chan_count = 0
        self.sig = False
        self.sig_count = 0
        self.seq = 0


class Prog:
    def __init__(self, nc):
        self.nc = nc
        self.ops = {e: [] for e in ENGS}
        self.chans = []
        self.all_ops = []
        self.fence_deps = []
        self.flip = 0
        self.dead = False

    def chan(self, name):
        c = Chan(name)
        self.chans.append(c)
        return c

    def res(self, name, loose=False):
        return Res(name, self.fence_deps, loose)

    def fence(self):
        if self.dead:
            return
        deps = []
        for e in ENGS:
            for op in reversed(self.ops[e]):
                if op.kind == "c":
                    deps.append(op)
                    break
        for c in self.chans:
            if c.last is not None:
                deps.append(c.last)
        self.fence_deps = deps

    def _add(self, op, reads, writes):
        deps = []
        for r in reads:
            deps.extend((x, r.loose) for x in r.writers)
        for w in writes:
            deps.extend((x, w.loose) for x in w.writers)
            deps.extend((x, w.loose) for x in w.readers)
        seen = {}
        best = {}
        for d, loose in deps:
            if d is op:
                continue
            if loose and d.eng == op.eng:
                continue
            if d.kind == "c":
                if d.eng not in best or d.seq > best[d.eng].seq:
                    best[d.eng] = d
            elif id(d) not in seen:
                seen[id(d)] = True
                op.deps.append(d)
        op.deps.extend(best.values())
        op.seq = len(self.ops[op.eng])
        for r in reads:
            if op.kind == "c":
                r.readers = [x for x in r.readers if not (x.kind == "c" and x.eng == op.eng)]
            r.readers.append(op)
        for w in writes:
            w.writers = [op]
            w.readers = []
        self.ops[op.eng].append(op)
        self.all_ops.append(op)
        return op

    def op(self, eng, fn, reads=(), writes=()):
        if self.dead:
            return self.ops[eng][-1]
        return self._add(Op(eng, fn, "c"), list(reads), list(writes))

    def dma(self, eng, chan, out, in_, reads=(), writes=()):
        if self.dead:
            return None

        def fn(e, out=out, in_=in_):
            return e.dma_start(out=out, in_=in_)
        op = Op(eng, fn, "d")
        op.chan = chan
        chan.count += 1
        op.chan_count = chan.count
        chan.last = op
        return self._add(op, list(reads), list(writes))

    def mm_group(self, out, pairs, psres, reads):
        if self.dead:
            return None
        n = len(pairs)
        last = None
        for i, (l, r) in enumerate(pairs):
            def fn(e, l=l, r=r, i=i):
                return e.matmul(out, lhsT=l, rhs=r, start=(i == 0), stop=(i == n - 1))
            last = self.op("pe", fn, reads=reads, writes=[psres] if i == 0 else [])
        psres.writers = [last]
        psres.readers = []
        return last

    def evac(self, out, in_, reads, writes, eng=None):
        if eng is None:
            self.flip ^= 1
            eng = "act" if self.flip else "dve"
        if eng == "act":
            return self.op("act", lambda e: e.activation(out=out, in_=in_, func=AF.Copy), reads=reads, writes=writes)
        return self.op("dve", lambda e: e.tensor_copy(out=out, in_=in_), reads=reads, writes=writes)

    def emit(self, final_chans=()):
        nc = self.nc
        for op in self.all_ops:
            for d in op.deps:
                if d.kind == "c":
                    if d.eng != op.eng:
                        d.sig = True
                    elif SAME_ENGINE_SYNC and d.eng != "pe":
                        d.sig = True
        for e in ENGS:
            cnt = 0
            for op in self.ops[e]:
                if op.kind == "c" and op.sig:
                    cnt += 1
                    op.sig_count = cnt
        with contextlib.ExitStack() as st:
            esem = {e: st.enter_context(nc.semaphore("sem_" + e)) for e in ENGS}
            nsem = 0
            for c in self.chans:
                if c.count > 0:
                    nsem += 1
                    c.sem = st.enter_context(nc.semaphore("ch%d_%s" % (nsem, c.name)))
            block = st.enter_context(nc.Block())

            def run(e, eng):
                waited = {}
                for op in self.ops[e]:
                    need = {}
                    for d in op.deps:
                        if d.kind == "c":
                            if d.eng == e and (e == "pe" or not SAME_ENGINE_SYNC):
                                continue
                            key = ("e", d.eng)
                            sem = esem[d.eng]
                            val = d.sig_count
                        else:
                            key = ("c", id(d.chan))
                            sem = d.chan.sem
                            val = 16 * d.chan_count
                        if val > need.get(key, (None, 0))[1]:
                            need[key] = (sem, val)
                    for key, (sem, val) in need.items():
                        if waited.get(key, 0) >= val:
                            continue
                        waited[key] = val
                        eng.wait_ge(sem, val)
                    ins = op.fn(eng)
                    if op.kind == "c":
                        if op.sig:
                            ins.then_inc(esem[e], 1)
                    else:
                        ins.then_inc(op.chan.sem, 16)
                if e == "sp":
                    for c in final_chans:
                        if c.count > 0:
                            eng.wait_ge(c.sem, 16 * c.count)

            @block.tensor
            def _(eng):
                run("pe", eng)

            @block.scalar
            def _(eng):
                run("act", eng)

            @block.vector
            def _(eng):
                run("dve", eng)

            @block.gpsimd
            def _(eng):
                run("pool", eng)

            @block.sync
            def _(eng):
                run("sp", eng)


class _Stop(Exception):
    pass


class Ring:
    def __init__(self, P, alloc, name, n, shape, dtype, chans=True):
        self.t = [alloc(name + str(i), shape, dtype) for i in range(n)]
        self.r = [P.res(name + str(i)) for i in range(n)]
        self.c = [P.chan(name + str(i)) for i in range(n)] if chans else [None] * n
        self.i = -1
        self.n = n

    def next(self):
        self.i = (self.i + 1) % self.n
        return self.t[self.i], self.r[self.i], self.c[self.i]


def ssl(start, n, step):
    return slice(start, start + step * (n - 1) + 1, step)


def alibi(n):
    return 2.0 ** (-8.0 * np.arange(1, n + 1, dtype=np.float64) / n)


A_SLOPES = alibi(24).reshape(3, 8)
B_SLOPES = alibi(16)


def build_program():
    nc = bass.Bass("TRN2", target_bir_lowering=False)
    P = Prog(nc)

    def din(name, shape):
        return nc.dram_tensor(name, list(shape), F32, kind="ExternalInput").ap()

    def dout(name, shape):
        return nc.dram_tensor(name, list(shape), F32, kind="ExternalOutput").ap()

    xp = din("xp", [NEXT, D])
    xs = din("xs", [128, D])
    cvec = din("cvec", [17, D])
    ca = [din("ca128", [16, 128, 2048]), din("ca512", [16, 512, 2048]), din("ca2048", [16, 2048, 2048])]
    cbk = din("cbk", [16, 128, 512])
    hmask_d = din("hmask", [128, 4])
    ident_d = din("ident", [128, 128])
    tab2_d = din("tab2", [128, 512])
    tabn_d = din("tabn", [128, 6 * 128])
    tsa_d = din("tsa", [128, 64 + 16 + 8])
    tsb_d = din("tsb", [128, 128])
    norm_mix_g = din("norm_mix_g", [2, D])
    norm_ffn_g = din("norm_ffn_g", [2, D])
    w_ada = din("w_ada", [2, D, 6 * D])
    b_ada = din("b_ada", [2, 6 * D])
    w_a_in = din("w_a_in", [D, 9216])
    w_a_out = din("w_a_out", [D, D])
    w_b_in = din("w_b_in", [D, 1536])
    b_sinks = din("b_sinks", [16])
    w_b_out = din("w_b_out", [D, D])
    w_ffn_gate = din("w_ffn_gate", [2, D, DFF])
    w_ffn_up = din("w_ffn_up", [2, D, DFF])
    w_ffn_down = din("w_ffn_down", [2, DFF, D])
    norm_final_g = din("norm_final_g", [D])

    y_p = dout("y_p", [2048, D])
    y_s = dout("y_s", [128, D])
    pa = [dout("pa128", [128, 2, 8, 128]), dout("pa512", [512, 2, 8, 128]), dout("pa2048", [2048, 2, 8, 128])]
    pb = dout("pb", [128, 512])
    sa = [dout("sa128", [128, 2, 8, 128]), dout("sa512", [128, 2, 8, 128]), dout("sa2048", [128, 2, 8, 128])]
    sbo = dout("sb", [128, 512])

    out_chans = []
    uniq = [0]
    kstop = int(os.environ.get("KSTOP", "0"))

    def ck(n):
        if kstop == n:
            P.dead = True

    class Scope:
        def __init__(self):
            self.st = contextlib.ExitStack()

        def sb(self, name, shape, dtype):
            uniq[0] += 1
            return self.st.enter_context(nc.sbuf_tensor("sb%d_%s" % (uniq[0], name), list(shape), dtype))

        def close(self):
            P.fence()
            self.st.close()

    g = Scope()
    psA = Ring(P, lambda n, s, d: nc.alloc_psum_tensor(n, s, d), "psA", 2, [128, 512], F32, chans=False)
    psT = Ring(P, lambda n, s, d: nc.alloc_psum_tensor(n, s, d), "psT", 2, [128, 1024], BF16, chans=False)
    psS = Ring(P, lambda n, s, d: nc.alloc_psum_tensor(n, s, d), "psS", 2, [128, 256], F32, chans=False)
    psO_t = [nc.alloc_psum_tensor("psO%d" % i, [128, 2, 128], F32) for i in range(2)]
    psO_r = [P.res("psO0"), P.res("psO1")]
    psO_i = [0]

    def next_psO():
        psO_i[0] ^= 1
        return psO_t[psO_i[0]][:], psO_r[psO_i[0]]

    identb = g.sb("identb", [128, 128], BF16)
    onesb = g.sb("onesb", [128, 128], BF16)
    tab2 = g.sb("tab2", [128, 512], F32)
    tabn = g.sb("tabn", [128, 768], F32)
    tsa = g.sb("tsa", [128, 88], F32)
    tsb = g.sb("tsb", [128, 128], F32)
    hmask = g.sb("hmask", [128, 4], F32)
    epst = g.sb("epst", [128, 1], F32)
    srep = [g.sb("srep_p", [128, 8, 128], BF16), g.sb("srep_s", [128, 8, 128], BF16)]
    esink = g.sb("esink", [128, 16], F32)
    Rc = P.res("consts")
    cc = P.chan("consts")
    Rid = P.res("ident")
    P.dma("pool", P.chan("ident"), identb[:], ident_d, writes=[Rid])
    P.dma("sp", cc, tab2[:], tab2_d, writes=[])
    P.dma("sp", cc, tabn[:], tabn_d, writes=[])
    P.dma("sp", cc, tsa[:], tsa_d, writes=[])
    P.dma("sp", cc, tsb[:], tsb_d, writes=[])
    P.dma("sp", cc, hmask[:], hmask_d, writes=[])
    last_c = P.dma("sp", cc, esink[:], b_sinks.partition_broadcast(128), writes=[])
    Rc.writers = [last_c]
    P.op("dve", lambda e: e.memset(onesb[:], 1.0), writes=[Rc], reads=[Rc, Rid])
    P.op("dve", lambda e: e.memset(epst[:], 1e-6), writes=[Rc], reads=[Rc])
    P.op("act", lambda e: e.activation(out=esink[:], in_=esink[:], func=AF.Exp), reads=[Rc], writes=[Rc])

    s0 = Scope()
    c_sb = s0.sb("c_sb", [17, D], F32)
    cs_bf = s0.sb("cs_bf", [17, D], BF16)
    siluT = s0.sb("siluT", [128, 8, 17], BF16)
    Rt = P.res("s0")
    ch0 = P.chan("s0")
    P.dma("sp", ch0, c_sb[:], cvec, writes=[Rt])
    P.op("act", lambda e: e.activation(out=cs_bf[:], in_=c_sb[:], func=AF.Silu), reads=[Rt], writes=[Rt])
    pt, ptr, _ = psT.next()
    for kt in range(8):
        P.op("pe", lambda e, kt=kt: e.transpose(out=pt[:, kt * 32:kt * 32 + 17], in_=cs_bf[:, kt * 128:(kt + 1) * 128],
                                                 identity=identb[0:17, 0:17]),
             reads=[Rt, Rc], writes=[ptr] if kt == 0 else [])
    ptr.writers = [P.ops["pe"][-1]]
    P.op("dve", lambda e: e.tensor_copy(out=siluT[:], in_=pt[:, 0:256].rearrange("p (k c) -> p k c", c=32)[:, :, 0:17]),
         reads=[ptr], writes=[Rt])
    Rsrep = P.res("srep")
    P.op("dve", lambda e: e.tensor_copy(out=srep[0][:], in_=siluT[:, :, 0:1].to_broadcast([128, 8, 128])),
         reads=[Rt], writes=[Rsrep])
    P.op("dve", lambda e: e.tensor_copy(out=srep[1][:].rearrange("p k (b t) -> p k b t", t=8),
                                         in_=siluT[:, :, 1:17].unsqueeze(3).to_broadcast([128, 8, 16, 8])),
         reads=[Rt, Rsrep], writes=[Rsrep])
    s0.close()
    ck(1)

    class Work:
        def __init__(self, scope, xs_=False):
            self.scrF = Ring(P, scope.sb, "scrF", 2, [128, D], F32, chans=False)
            self.scrB = Ring(P, scope.sb, "scrB", 2, [128, D], BF16, chans=False)
            self.stat = Ring(P, scope.sb, "stat", 4, [128, 2], F32, chans=False)
            self.xstg = Ring(P, scope.sb, "xstg", 2, [128, D], F32) if xs_ else None

    def set_of(blk_is_sample):
        return 1 if blk_is_sample else 0

    def mod_tiles(scope, layer, chunks, gains):
        out = {}
        tls = {}
        for (j, kind) in chunks:
            tls[j] = ([scope.sb("mod%d_%d_%d" % (layer, j, s), [128, D], F32) for s in range(2)],
                      [P.res("mod%d_%d_%d" % (layer, j, s)) for s in range(2)])
        inner = Scope()
        wada = Ring(P, inner.sb, "wada", 2, [128, 8, 512], BF16)
        bada = Ring(P, inner.sb, "bada", 1, [1, 512], BF16)
        if any(k == "A" for _, k in chunks):
            gbc = Ring(P, inner.sb, "gbc", 1, [128, D], F32)
        for (j, kind) in chunks:
            tl, rs = tls[j]
            if kind == "A":
                gt, gr, gc = gbc.next()
                P.dma("sp", gc, gt[:], gains[layer].partition_broadcast(128), writes=[gr])
            for half in range(2):
                c0 = j * D + half * 512
                wt, wr, wc = wada.next()
                P.dma("pool", wc, wt[:], w_ada[layer, :, c0:c0 + 512].rearrange("(kt p) n -> p kt n", p=128), writes=[wr])
                bt, br, bc = bada.next()
                P.dma("pool", bc, bt[:], b_ada[layer:layer + 1, c0:c0 + 512], writes=[br])
                for s in range(2):
                    ps, psr, _ = psA.next()
                    pairs = [(srep[s][:, kt, :], wt[:, kt, :]) for kt in range(8)] + [(onesb[0:1, :], bt[0:1, :])]
                    P.mm_group(ps[:], pairs, psr, [Rsrep, wr, br, Rc])
                    dst = tl[s][:, half * 512:(half + 1) * 512]
                    if kind == "A":
                        P.op("dve", lambda e, dst=dst, ps=ps, gt=gt, half=half: e.scalar_tensor_tensor(
                            out=dst, in0=ps[:], scalar=1.0, in1=gt[:, half * 512:(half + 1) * 512],
                            op0=ALU.add, op1=ALU.mult), reads=[psr, gr], writes=[rs[s]] if half == 0 else [])
                        if half == 1:
                            rs[s].writers = [P.ops["dve"][-1]]
                    else:
                        o = P.evac(dst, ps[:], [psr], [rs[s]] if half == 0 else [])
                        if half == 1:
                            rs[s].writers = rs[s].writers + [o]
            for s in range(2):
                out[(s, j)] = (tl[s], rs[s])
        inner.close()
        return out

    def norm_block(wk, xin, xin_r, Am, Bm, dstT, dst_r, col0):
        st, sr = wk.stat.next()[:2]
        jb, jr = wk.scrB.next()[:2]
        P.op("act", lambda e: e.activation(out=jb[:], in_=xin, func=AF.Square, scale=1.0 / 32, accum_out=st[:, 0:1]),
             reads=[xin_r], writes=[jr, sr])
        P.op("act", lambda e: e.activation(out=st[:, 1:2], in_=st[:, 0:1], func=AF.Sqrt, bias=epst[:], scale=1.0),
             reads=[sr, Rc], writes=[sr])
        P.op("dve", lambda e: e.reciprocal(out=st[:, 1:2], in_=st[:, 1:2]), reads=[sr], writes=[sr])
        tf, tr = wk.scrF.next()[:2]
        P.op("dve", lambda e: e.scalar_tensor_tensor(out=tf[:], in0=xin, scalar=st[:, 1:2], in1=Am[0][:],
                                                      op0=ALU.mult, op1=ALU.mult), reads=[xin_r, sr, Am[1]], writes=[tr])
        hb, hr = wk.scrB.next()[:2]
        P.op("pool", lambda e: e.tensor_tensor(out=hb[:], in0=tf[:], in1=Bm[0][:], op=ALU.add), reads=[tr, Bm[1]], writes=[hr])
        pt, ptr, _ = psT.next()
        for kt in range(8):
            P.op("pe", lambda e, kt=kt: e.transpose(out=pt[:, kt * 128:(kt + 1) * 128], in_=hb[:, kt * 128:(kt + 1) * 128],
                                                     identity=identb[:]),
                 reads=[hr, Rc], writes=[ptr] if kt == 0 else [])
        ptr.writers = [P.ops["pe"][-1]]
        P.evac(dstT[:, :, col0:col0 + 128], pt[:].rearrange("p (k t) -> p k t", t=128), [ptr], [dst_r])

    actA = g.sb("actA", [128, 8, NTOK], BF16)
    actA_r = [P.res("actA%d" % i) for i in range(18)]

    sX = Scope()
    qTs = sX.sb("qTs", [128, 24, 128], BF16)
    acc_s = sX.sb("acc_s", [128, 2, 8, 128], F32)
    sL0 = Scope()
    hT = sL0.sb("hT_ext", [128, 8, NEXT + 128], BF16)
    hT_r = [P.res("hT%d" % i) for i in range(34)]
    m1 = Scope()
    wk = Work(m1, True)
    mods = mod_tiles(m1, 0, [(0, "B"), (1, "A")], norm_mix_g)
    for eb in range(34):
        xt, xr_, xc = wk.xstg.next()
        src = xp[eb * 128:(eb + 1) * 128, :] if eb < 33 else xs
        P.dma("sp", xc, xt[:], src, writes=[xr_])
        s = set_of(eb == 33)
        norm_block(wk, xt[:], xr_, mods[(s, 1)], mods[(s, 0)], hT, hT_r[eb], eb * 128)
    m1.close()
    ck(2)

    qT = sL0.sb("qT", [128, NOWN], BF16)
    kT = sL0.sb("kT", [128, NEXT + 128], BF16)
    vv = sL0.sb("vv", [128, 50, 128], BF16)
    acc = sL0.sb("acc", [128, 2, NOWN], F32)
    tabh = sL0.sb("tabh", [128, 256], F32)
    tabnh = sL0.sb("tabnh", [128, 128], F32)
    w3 = Ring(P, sL0.sb, "w3", 2, [128, 8, 384], BF16)
    scS = Ring(P, sL0.sb, "scS", 3, [128, 256], F32, chans=False)
    pP = Ring(P, sL0.sb, "pP", 4, [128, 256], BF16, chans=False)
    ostg = Ring(P, sL0.sb, "ostg", 3, [128, 256], F32)
    for c_ in ostg.c:
        out_chans.append(c_)
    qT_r, kT_r, vv_r, tab_r, tabn_r = [P.res(n) for n in ("qT", "kT", "vv", "tabh", "tabnh")]
    acc_rr = {(b_, q_): P.res("acc%d_%d" % (b_, q_)) for b_ in range(17) for q_ in range(16)}
    acc_all = list(acc_rr.values())
    qTs_r = P.res("qTs")
    acc_s_rr = {(b_, q_): P.res("accs%d_%d" % (b_, q_)) for b_ in range(16) for q_ in range(8)}
    acc_s_all = list(acc_s_rr.values())

    def acc_res(t0, n, d_):
        blocks = range(t0 // 128, (t0 + d_ * (n - 1)) // 128 + 1)
        rs_ = sorted(set((t0 + d_ * i) % 16 for i in range(min(n, 16))))
        return [acc_rr[(b_, q_)] for b_ in blocks for q_ in rs_]
    hT_all = hT_r

    def tok_tiles(lo, hi):
        t = []
        while lo < hi:
            w = min(512, hi - lo)
            t.append((lo, w))
            lo += w
        return t

    scale_a = 128 ** -0.5
    for h in range(8):
        for gi, (W, d) in enumerate(A_GROUPS):
            gh = gi * 8 + h
            M = NEXT // d
            mq0 = 2048 // d
            wt, wr, wc = w3.next()
            for j in range(3):
                c0 = gi * 3072 + j * 1024 + h * 128
                P.dma("pool", wc, wt[:, :, j * 128:(j + 1) * 128],
                      w_a_in[:, c0:c0 + 128].rearrange("(kt p) n -> p kt n", p=128), writes=[wr] if j == 0 else [])
            wr.writers = [wc.last]
            if gh == 0:
                ck(30)
            for (lo, w) in tok_tiles(2048, NEXT + 128):
                ps, psr, _ = psA.next()
                rd = [hT_r[b] for b in range(lo // 128, (lo + w) // 128)]
                P.mm_group(ps[:, 0:w], [(wt[:, kt, 0:128], hT[:, kt, lo:lo + w]) for kt in range(8)], psr, rd + [wr])
                if gh == 0 and lo == 2048:
                    ck(301)
                if lo + w <= NEXT:
                    P.evac(qT[:, lo - 2048:lo - 2048 + w], ps[:, 0:w], [psr], [qT_r])
                    if gh == 0 and lo == 2048:
                        ck(302)
                    if gh == 0 and lo == 3584:
                        ck(303)
                else:
                    wo = NEXT - lo
                    P.evac(qT[:, lo - 2048:lo - 2048 + wo], ps[:, 0:wo], [psr], [qT_r], eng="dve")
                    P.evac(qTs[:, gh, :], ps[:, wo:wo + 128], [psr], [qTs_r], eng="dve")
            if gh == 0:
                ck(31)
            klo = 0 if gi == 2 else 1536
            for (lo, w) in tok_tiles(klo, NEXT + 128):
                ps, psr, _ = psA.next()
                rd = [hT_r[b] for b in range(lo // 128, (lo + w) // 128)]
                P.mm_group(ps[:, 0:w], [(wt[:, kt, 128:256], hT[:, kt, lo:lo + w]) for kt in range(8)], psr, rd + [wr])
                P.evac(kT[:, lo:lo + w], ps[:, 0:w], [psr], [kT_r])
            if gh == 0:
                ck(32)
            c_lo = (mq0 - 128) // 128
            n_c = -(-M // 128)
            vslot = {}
            e_tail0 = NEXT - W
            for r in range(d):
                for c in range(c_lo, n_c):
                    k0 = 128 * c
                    nk = min(128, M - k0)
                    slot = len(vslot)
                    vslot[(r, c)] = slot
                    cols = ssl(r + d * k0, nk, d)
                    kk0 = max(0, -(-(e_tail0 - r - d * k0) // d))
                    is_out = kk0 < nk
                    ps, psr, _ = psA.next()
                    blocks = sorted(set(range((r + d * k0) // 128, (r + d * (k0 + nk - 1)) // 128 + 1)))
                    rd = [hT_r[b] for b in blocks] + [wr]
                    if is_out:
                        P.mm_group(ps[0:nk, 0:256], [(hT[:, kt, cols], wt[:, kt, 128:384]) for kt in range(8)], psr, rd)
                        og, orr, oc = ostg.next()
                        P.op("dve", lambda e, og=og, ps=ps, nk=nk: e.tensor_copy(out=og[0:nk, :], in_=ps[0:nk, 0:256]),
                             reads=[psr], writes=[orr])
                        P.op("act", lambda e, og=og, nk=nk, slot=slot: e.activation(out=vv[0:nk, slot, :], in_=og[0:nk, 128:256], func=AF.Copy),
                             reads=[orr], writes=[vv_r])
                        r0 = d * (k0 + kk0) + r - e_tail0
                        cnt = nk - kk0
                        P.dma("sp", oc, pa[gi][r0:r0 + d * (cnt - 1) + 1:d, :, h, :],
                              og[kk0:nk, :].rearrange("p (a b) -> p a b", a=2), reads=[orr])
                    else:
                        P.mm_group(ps[0:nk, 0:128], [(hT[:, kt, cols], wt[:, kt, 256:384]) for kt in range(8)], psr, rd)
                        P.evac(vv[0:nk, slot, :], ps[0:nk, 0:128], [psr], [vv_r])
            if gh == 0:
                ck(33)
            ps, psr, _ = psA.next()
            P.mm_group(ps[:, 0:256], [(hT[:, kt, NEXT:NEXT + 128], wt[:, kt, 128:384]) for kt in range(8)], psr, [hT_r[33], wr])
            vs_slot = len(vslot)
            og, orr, oc = ostg.next()
            P.op("dve", lambda e, og=og, ps=ps: e.tensor_copy(out=og[:, :], in_=ps[:, 0:256]), reads=[psr], writes=[orr])
            P.op("act", lambda e, og=og, vs_slot=vs_slot: e.activation(out=vv[:, vs_slot, :], in_=og[:, 128:256], func=AF.Copy),
                 reads=[orr], writes=[vv_r])
            P.dma("sp", oc, sa[gi][:, :, h, :], og[:, :].rearrange("p (a b) -> p a b", a=2), reads=[orr])
            if gh == 0:
                ck(34)
            sl = float(-A_SLOPES[gi, h] * d)
            P.op("dve", lambda e, sl=sl: e.scalar_tensor_tensor(out=tabh[:], in0=tab2[:, 0:256], scalar=sl, in1=tab2[:, 256:512],
                                                                 op0=ALU.mult, op1=ALU.add), reads=[Rc], writes=[tab_r])
            sln = float(-A_SLOPES[gi, h])
            P.op("dve", lambda e, sln=sln, gi=gi: e.scalar_tensor_tensor(
                out=tabnh[:], in0=tabn[:, gi * 256:gi * 256 + 128], scalar=sln, in1=tabn[:, gi * 256 + 128:gi * 256 + 256],
                op0=ALU.mult, op1=ALU.add), reads=[Rc], writes=[tabn_r])
            if gh == 0:
                ck(35)
            chunks = []
            for r in range(d):
                for c in range(mq0 // 128, n_c):
                    q0 = 128 * c
                    nq = min(128, M - q0)

                    def halo_col(k0_, nk_, r=r):
                        nh = -(-(2176 - r - d * k0_) // d)
                        if nh <= 0:
                            return None
                        if nh >= nk_:
                            return 0
                        return {32: 1, 8: 2}[nh]
                    chunks.append(dict(
                        nq=nq, qcols=ssl(r + d * q0 - 2048, nq, d), kprev=ssl(r + d * (q0 - 128), 128, d),
                        kcur=ssl(r + d * q0, nq, d), hc_prev=halo_col(q0 - 128, 128), hc_cur=halo_col(q0, nq),
                        sp_=vslot[(r, c - 1)], sc_=vslot[(r, c)], ars=acc_res(r + d * q0 - 2048, nq, d)))

            def stage1(ch):
                nq, qcols, kprev, kcur = ch["nq"], ch["qcols"], ch["kprev"], ch["kcur"]
                S, Sr = psS.next()[:2]
                ch["S"], ch["Sr"] = S, Sr
                P.op("pe", lambda e: e.matmul(S[:, 0:nq], lhsT=kT[:, kprev], rhs=qT[:, qcols], start=True, stop=True),
                     reads=[kT_r, qT_r], writes=[Sr])
                b_ = P.op("pe", lambda e: e.matmul(S[0:nq, 128:128 + nq], lhsT=kT[:, kcur], rhs=qT[:, qcols], start=True, stop=True),
                          reads=[kT_r, qT_r], writes=[])
                Sr.writers = [b_]

            def stage2(ch):
                nq, S, Sr, hc_prev, hc_cur = ch["nq"], ch["S"], ch["Sr"], ch["hc_prev"], ch["hc_cur"]
                sc, scr = scS.next()[:2]
                pp, ppr = pP.next()[:2]
                ch["pp"], ch["ppr"] = pp, ppr
                if nq == 128 and hc_prev is None and hc_cur is None:
                    P.op("dve", lambda e: e.scalar_tensor_tensor(out=sc[:], in0=S[:, 0:256], scalar=scale_a, in1=tabh[:],
                                                                  op0=ALU.mult, op1=ALU.add), reads=[Sr, tab_r], writes=[scr])
                    P.op("act", lambda e: e.activation(out=pp[:], in_=sc[:], func=AF.Exp), reads=[scr], writes=[ppr])
                    return
                P.op("dve", lambda e: e.scalar_tensor_tensor(out=sc[:, 0:nq], in0=S[:, 0:nq], scalar=scale_a, in1=tabh[:, 0:nq],
                                                              op0=ALU.mult, op1=ALU.add), reads=[Sr, tab_r], writes=[scr])
                P.op("dve", lambda e: e.scalar_tensor_tensor(out=sc[0:nq, 128:128 + nq], in0=S[0:nq, 128:128 + nq], scalar=scale_a,
                                                              in1=tabh[0:nq, 128:128 + nq], op0=ALU.mult, op1=ALU.add),
                     reads=[Sr, tab_r, scr], writes=[scr])
                if hc_prev is not None:
                    P.op("act", lambda e: e.activation(out=pp[:, 0:nq], in_=sc[:, 0:nq], func=AF.Exp, bias=hmask[:, hc_prev:hc_prev + 1]),
                         reads=[scr, Rc], writes=[ppr])
                else:
                    P.op("act", lambda e: e.activation(out=pp[:, 0:nq], in_=sc[:, 0:nq], func=AF.Exp), reads=[scr], writes=[ppr])
                if hc_cur is not None:
                    P.op("act", lambda e: e.activation(out=pp[0:nq, 128:128 + nq], in_=sc[0:nq, 128:128 + nq], func=AF.Exp,
                                                       bias=hmask[0:nq, hc_cur:hc_cur + 1]), reads=[scr, Rc, ppr], writes=[ppr])
                else:
                    P.op("act", lambda e: e.activation(out=pp[0:nq, 128:128 + nq], in_=sc[0:nq, 128:128 + nq], func=AF.Exp),
                         reads=[scr, ppr], writes=[ppr])

            def stage3(ch):
                nq, pp, ppr, qcols, ars = ch["nq"], ch["pp"], ch["ppr"], ch["qcols"], ch["ars"]
                O, Or = next_psO()
                P.mm_group(O[:, 0, 0:nq], [(vv[:, ch["sp_"], :], pp[:, 0:nq]), (vv[0:nq, ch["sc_"], :], pp[0:nq, 128:128 + nq])],
                           Or, [vv_r, ppr])
                P.mm_group(O[:, 1, 0:nq], [(onesb[:, :], pp[:, 0:nq]), (onesb[0:nq, :], pp[0:nq, 128:128 + nq])],
                           P.res("dummy"), [ppr, Rc])
                Or.writers = [P.ops["pe"][-1]]
                if gi == 0:
                    P.evac(acc[:, :, qcols], O[:, :, 0:nq], [Or], ars, eng="dve")
                else:
                    P.op("dve", lambda e: e.tensor_tensor(out=acc[:, :, qcols], in0=acc[:, :, qcols], in1=O[:, :, 0:nq], op=ALU.add),
                         reads=[Or] + ars, writes=ars)

            for i in range(len(chunks) + LOOK):
                if i < len(chunks):
                    stage1(chunks[i])
                    stage2(chunks[i])
                if i >= LOOK:
                    stage3(chunks[i - LOOK])
            if gh == 0:
                ck(36)
            if gh == 8:
                ck(38)
            if gh == 16:
                ck(39)
            S, Sr, _ = psS.next()
            P.mm_group(S[:, 0:128], [(kT[:, NEXT:NEXT + 128], qTs[:, gh, :])], Sr, [kT_r, qTs_r])
            sc, scr = scS.next()[:2]
            pp, ppr = pP.next()[:2]
            P.op("dve", lambda e, sc=sc, S=S: e.scalar_tensor_tensor(
                out=sc[:, 0:128], in0=S[:, 0:128], scalar=scale_a, in1=tabnh[:], op0=ALU.mult, op1=ALU.add),
                reads=[Sr, tabn_r], writes=[scr])
            P.op("act", lambda e, sc=sc, pp=pp: e.activation(out=pp[:, 0:128], in_=sc[:, 0:128], func=AF.Exp),
                 reads=[scr], writes=[ppr])
            O, Or = next_psO()
            P.mm_group(O[:, 0, :], [(vv[:, vs_slot, :], pp[:, 0:128])], Or, [vv_r, ppr])
            P.mm_group(O[:, 1, :], [(onesb[:, :], pp[:, 0:128])], P.res("dummy"), [ppr, Rc])
            Or.writers = [P.ops["pe"][-1]]
            if gi == 0:
                P.evac(acc_s[:, :, h, :], O[:, :, :], [Or], acc_s_all, eng="dve")
            else:
                P.op("dve", lambda e, O=O, h=h: e.tensor_tensor(out=acc_s[:, :, h, :], in0=acc_s[:, :, h, :], in1=O[:, :, :],
                                                              op=ALU.add), reads=[Or] + acc_s_all, writes=acc_s_all)
        P.op("dve", lambda e: e.tensor_scalar_max(out=acc[:, 1, :], in0=acc[:, 1, :], scalar1=1e-30), reads=acc_all, writes=acc_all)
        P.op("dve", lambda e: e.reciprocal(out=acc[:, 1, :], in_=acc[:, 1, :]), reads=acc_all, writes=acc_all)
        P.op("dve", lambda e, h=h: e.tensor_tensor(out=actA[:, h, 0:NOWN], in0=acc[:, 0, :], in1=acc[:, 1, :], op=ALU.mult),
             reads=acc_all, writes=actA_r[0:17] + acc_all)
        if h == 0:
            ck(3)
    sL0.close()
    ck(4)

    sS = Scope()
    kc = Ring(P, sS.sb, "kc", 3, [128, 1024], BF16)
    vc = Ring(P, sS.sb, "vc", 3, [128, 1024], BF16)
    KT = Ring(P, sS.sb, "KT", 2, [128, 8, 128], BF16, chans=False)
    scS2 = Ring(P, sS.sb, "scS2", 2, [128, 128], F32, chans=False)
    pP2 = Ring(P, sS.sb, "pP2", 2, [128, 128], BF16, chans=False)
    tiles = [(0, 0, 1, 8, 0)]
    tiles += [(1, r, 4, 2, 64) for r in range(4)]
    tiles += [(2, i, 16, 1, 80) for i in range(8)]
    for b in range(16):
        for (gi, r0, rs, nq, tc0) in tiles:
            kt_, kr, kch = kc.next()
            vt_, vr, vch = vc.next()
            nrows = ca[gi].shape[1]
            P.dma("pool", kch, kt_[:], ca[gi][b, ssl(r0, 128, rs), 0:1024], writes=[kr])
            P.dma("pool", vch, vt_[:], ca[gi][b, ssl(r0, 128, rs), 1024:2048], writes=[vr])
            pt, ptr, _ = psT.next()
            for h in range(8):
                P.op("pe", lambda e, h=h, pt=pt, kt_=kt_: e.transpose(out=pt[:, h * 128:(h + 1) * 128], in_=kt_[:, h * 128:(h + 1) * 128],
                                                                     identity=identb[:]),
                     reads=[kr, Rc], writes=[ptr] if h == 0 else [])
            ptr.writers = [P.ops["pe"][-1]]
            KTt, KTr = KT.next()[:2]
            P.evac(KTt[:].rearrange("p a b -> p (a b)"), pt[:], [ptr], [KTr])
            S, Sr, _ = psS.next()
            if nq == 8:
                qsl = slice(b * 8, b * 8 + 8)
            elif nq == 2:
                qsl = slice(b * 8 + r0, b * 8 + r0 + 5, 4)
            else:
                qsl = slice(b * 8 + r0, b * 8 + r0 + 1)
            for h in range(8):
                o_ = P.op("pe", lambda e, h=h, S=S, KTt=KTt, qsl=qsl, gi=gi, nq=nq: e.matmul(
                    S[:, h * nq:(h + 1) * nq], lhsT=KTt[:, h, :], rhs=qTs[:, gi * 8 + h, qsl], start=True, stop=True),
                    reads=[KTr, qTs_r], writes=[Sr] if h == 0 else [])
            Sr.writers = [o_]
            sc, scr = scS2.next()[:2]
            pp, ppr = pP2.next()[:2]
            nn = 8 * nq
            P.op("dve", lambda e, sc=sc, S=S, nn=nn, tc0=tc0: e.scalar_tensor_tensor(
                out=sc[:, 0:nn], in0=S[:, 0:nn], scalar=scale_a, in1=tsa[:, tc0:tc0 + nn], op0=ALU.mult, op1=ALU.add),
                reads=[Sr, Rc], writes=[scr])
            P.op("act", lambda e, sc=sc, pp=pp, nn=nn: e.activation(out=pp[:, 0:nn], in_=sc[:, 0:nn], func=AF.Exp),
                 reads=[scr], writes=[ppr])
            O, Or = next_psO()
            Of = O.rearrange("p a b -> p (a b)")
            for h in range(8):
                o_ = P.op("pe", lambda e, h=h, Of=Of, vt_=vt_, pp=pp, nq=nq: e.matmul(
                    Of[:, h * nq:(h + 1) * nq], lhsT=vt_[:, h * 128:(h + 1) * 128], rhs=pp[:, h * nq:(h + 1) * nq],
                    start=True, stop=True), reads=[vr, ppr], writes=[Or] if h == 0 else [])
            o_ = P.op("pe", lambda e, Of=Of, pp=pp, nn=nn: e.matmul(Of[:, 128:128 + nn], lhsT=onesb[:, :], rhs=pp[:, 0:nn],
                                                                      start=True, stop=True), reads=[ppr, Rc], writes=[])
            Or.writers = [o_]
            dst = acc_s[:, :, :, qsl]
            src = O[:, :, 0:nn].rearrange("p a (h q) -> p a h q", q=nq)
            if nq == 8:
                srs = [acc_s_rr[(b, i_)] for i_ in range(8)]
            elif nq == 2:
                srs = [acc_s_rr[(b, r0)], acc_s_rr[(b, r0 + 4)]]
            else:
                srs = [acc_s_rr[(b, r0)]]
            P.op("dve", lambda e, dst=dst, src=src: e.tensor_tensor(out=dst, in0=dst, in1=src, op=ALU.add),
                 reads=[Or] + srs, writes=srs)
    P.op("dve", lambda e: e.reciprocal(out=acc_s[:, 1], in_=acc_s[:, 1]), reads=acc_s_all, writes=acc_s_all)
    P.op("dve", lambda e: e.tensor_tensor(out=actA[:, :, NOWN:NTOK], in0=acc_s[:, 0], in1=acc_s[:, 1], op=ALU.mult),
         reads=acc_s_all, writes=[actA_r[17]] + acc_s_all)
    sS.close()
    sX.close()
    ck(5)

    xr = g.sb("xr", [128, 18, D], F32)
    xr_r = [P.res("xr%d" % i) for i in range(18)]

    def out_proj_residual(layer, w_out, gate_j, blocks, x_from_dram):
        sc_ = Scope()
        wo = sc_.sb("wo", [128, 8, D], BF16)
        wor = P.res("wo")
        woc = P.chan("wo%d" % layer)
        P.dma("pool", woc, wo[:], w_out.rearrange("(kt p) n -> p kt n", p=128), writes=[wor])
        mods = mod_tiles(sc_, layer, [(gate_j, "B")], None)
        wk = Work(sc_, x_from_dram)
        scrF = wk.scrF
        for bi in blocks:
            s = set_of(bi == 17)
            G, Gr = mods[(s, gate_j)]
            if x_from_dram:
                xt, xtr, xc = wk.xstg.next()
                src = xp[(16 + bi) * 128:(17 + bi) * 128, :] if bi < 17 else xs
                P.dma("sp", xc, xt[:], src, writes=[xtr])
                xin, xin_r = xt[:], xtr
            else:
                xin, xin_r = xr[:, bi, :], xr_r[bi]
            tf, tr = scrF.next()[:2]
            for half in range(2):
                ps, psr, _ = psA.next()
                P.mm_group(ps[:], [(actA[:, kt, bi * 128:(bi + 1) * 128], wo[:, kt, half * 512:(half + 1) * 512]) for kt in range(8)],
                           psr, [actA_r[bi], wor])
                o_ = P.op("dve", lambda e, tf=tf, ps=ps, G=G, half=half: e.tensor_tensor(
                    out=tf[:, half * 512:(half + 1) * 512], in0=ps[:], in1=G[:, half * 512:(half + 1) * 512], op=ALU.mult),
                    reads=[psr, Gr], writes=[tr] if half == 0 else [])
            tr.writers = [o_]
            P.op("pool", lambda e, bi=bi, xin=xin, tf=tf: e.tensor_tensor(out=xr[:, bi, :], in0=xin, in1=tf[:], op=ALU.add),
                 reads=[xin_r, tr], writes=[xr_r[bi]])
        sc_.close()

    def ffn(layer, blocks):
        b_lo, b_hi = blocks[0], blocks[-1] + 1
        t_lo, t_hi = b_lo * 128, b_hi * 128
        m2 = Scope()
        mods = mod_tiles(m2, layer, [(3, "B"), (4, "A")], norm_ffn_g)
        wk = Work(m2)
        for bi in blocks:
            s = set_of(bi == 17)
            norm_block(wk, xr[:, bi, :], xr_r[bi], mods[(s, 4)], mods[(s, 3)], actA, actA_r[bi], bi * 128)
        m2.close()
        f = Scope()
        gmods = mod_tiles(f, layer, [(5, "B")], None)
        scrF = Ring(P, f.sb, "scrF", 2, [128, D], F32, chans=False)
        ncg = -(-NCT // FFN_GROUPS)
        hid = f.sb("hid", [128, ncg, NTOK], BF16)
        hid_r = P.res("hid")
        wd = f.sb("wd", [128, ncg, D], BF16)
        wd_r = P.res("wd")
        wd_c = P.chan("wd%d" % layer)
        wgu = Ring(P, f.sb, "wgu", 3, [128, 8, 256], BF16)
        sg = Ring(P, f.sb, "sg", 2, [128, 512], BF16, chans=False)
        for c0 in range(0, NCT, ncg):
            cn = min(ncg, NCT - c0)
            P.dma("pool", wd_c, wd[:, 0:cn, :], w_ffn_down[layer, c0 * 128:(c0 + cn) * 128, :].rearrange("(c p) n -> p c n", p=128),
                  writes=[wd_r])
            for ci in range(cn):
                c = c0 + ci
                wt, wr, wc = wgu.next()
                P.dma("pool", wc, wt[:, :, 0:128], w_ffn_gate[layer, :, c * 128:(c + 1) * 128].rearrange("(kt p) n -> p kt n", p=128),
                      writes=[wr])
                P.dma("pool", wc, wt[:, :, 128:256], w_ffn_up[layer, :, c * 128:(c + 1) * 128].rearrange("(kt p) n -> p kt n", p=128),
                      writes=[])
                wr.writers = [wc.last]
                for (lo, w) in tok_tiles(t_lo, t_hi):
                    rd = [actA_r[b] for b in range(lo // 128, (lo + w) // 128)] + [wr]
                    pg, pgr, _ = psA.next()
                    P.mm_group(pg[:, 0:w], [(wt[:, kt, 0:128], actA[:, kt, lo:lo + w]) for kt in range(8)], pgr, rd)
                    pu, pur, _ = psA.next()
                    P.mm_group(pu[:, 0:w], [(wt[:, kt, 128:256], actA[:, kt, lo:lo + w]) for kt in range(8)], pur, rd)
                    st_, str_ = sg.next()[:2]
                    P.op("act", lambda e, st_=st_, pg=pg, w=w: e.activation(out=st_[:, 0:w], in_=pg[:, 0:w], func=AF.Silu),
                         reads=[pgr], writes=[str_])
                    P.op("dve", lambda e, st_=st_, pu=pu, w=w, ci=ci, lo=lo: e.tensor_tensor(
                        out=hid[:, ci, lo:lo + w], in0=st_[:, 0:w], in1=pu[:, 0:w], op=ALU.mult),
                        reads=[str_, pur], writes=[hid_r])
            for bi in blocks:
                s = set_of(bi == 17)
                G, Gr = gmods[(s, 5)]
                tf, tr = scrF.next()[:2]
                for half in range(2):
                    ps, psr, _ = psA.next()
                    P.mm_group(ps[:], [(hid[:, ci, bi * 128:(bi + 1) * 128], wd[:, ci, half * 512:(half + 1) * 512]) for ci in range(cn)],
                               psr, [hid_r, wd_r])
                    o_ = P.op("dve", lambda e, tf=tf, ps=ps, G=G, half=half: e.tensor_tensor(
                        out=tf[:, half * 512:(half + 1) * 512], in0=ps[:], in1=G[:, half * 512:(half + 1) * 512], op=ALU.mult),
                        reads=[psr, Gr], writes=[tr] if half == 0 else [])
                tr.writers = [o_]
                P.op("pool", lambda e, bi=bi, tf=tf: e.tensor_tensor(out=xr[:, bi, :], in0=xr[:, bi, :], in1=tf[:], op=ALU.add),
                     reads=[tr, xr_r[bi]], writes=[xr_r[bi]])
        f.close()

    out_proj_residual(0, w_a_out[:, :], 2, list(range(18)), True)
    ck(6)
    ffn(0, list(range(18)))
    ck(7)

    m1 = Scope()
    mods = mod_tiles(m1, 1, [(0, "B"), (1, "A")], norm_mix_g)
    wk = Work(m1)
    for bi in range(18):
        s = set_of(bi == 17)
        norm_block(wk, xr[:, bi, :], xr_r[bi], mods[(s, 1)], mods[(s, 0)], actA, actA_r[bi], bi * 128)
    m1.close()
    ck(8)

    sB = Scope()
    actB = sB.sb("actB", [128, 8, NTOK], BF16)
    actB_r = [P.res("actB%d" % i) for i in range(18)]
    sBs = Scope()
    qsd = sBs.sb("qsd", [128, 16, 128], BF16)
    qsd_r = P.res("qsd")
    accB = sBs.sb("accB", [128, 2, 8, 128], F32)
    accB_rr = {(b_, a_): P.res("accB%d_%d" % (b_, a_)) for b_ in range(16) for a_ in range(2)}
    accB_all = list(accB_rr.values())
    sBi = Scope()
    qB = sBi.sb("qB", [128, 2, NTOK], BF16)
    kB = sBi.sb("kB", [128, NTOK], BF16)
    vB = sBi.sb("vB", [128, 18, 128], BF16)
    qB_r, kB_r, vB_r = P.res("qB"), P.res("kB"), P.res("vB")
    wq = Ring(P, sBi.sb, "wq", 1, [128, 8, 512], BF16)
    ostg2 = Ring(P, sBi.sb, "ostg2", 1, [128, 512], F32)
    for c_ in ostg2.c:
        out_chans.append(c_)
    tabB = sBi.sb("tabB", [128, 256], F32)
    tabnB = sBi.sb("tabnB", [128, 128], F32)
    tabB_r, tabnB_r = P.res("tabB"), P.res("tabnB")
    scS = Ring(P, sBi.sb, "scSb", 3, [128, 256], F32, chans=False)
    pP = Ring(P, sBi.sb, "pPb", 4, [128, 256], BF16, chans=False)
    den = Ring(P, sBi.sb, "den", 2, [128, 128], F32, chans=False)
    wqd = Ring(P, sBi.sb, "wqd", 2, [128, 8, 128], BF16)
    scale_b = 64 ** -0.5

    wt, wr, wc = wq.next()
    P.dma("pool", wc, wt[:], w_b_in[:, 1024:1536].rearrange("(kt p) n -> p kt n", p=128), writes=[wr])
    for bi, dst in ((16, pb), (17, sbo)):
        ps, psr, _ = psA.next()
        P.mm_group(ps[:], [(actA[:, kt, bi * 128:(bi + 1) * 128], wt[:, kt, :]) for kt in range(8)], psr, [actA_r[bi], wr])
        og, orr, oc = ostg2.next()
        P.evac(og[:], ps[:], [psr], [orr])
        P.dma("sp", oc, dst, og[:], reads=[orr])

    for hk in range(4):
        wt, wr, wc = wq.next()
        P.dma("pool", wc, wt[:, :, 0:256], w_b_in[:, hk * 256:(hk + 1) * 256].rearrange("(kt p) n -> p kt n", p=128), writes=[wr])
        for j in range(2):
            P.dma("pool", wc, wt[:, :, 256 + 64 * j:320 + 64 * j],
                  w_b_in[:, 1024 + hk * 64:1024 + (hk + 1) * 64].rearrange("(kt p) n -> p kt n", p=128), writes=[])
            P.dma("pool", wc, wt[:, :, 384 + 64 * j:448 + 64 * j],
                  w_b_in[:, 1280 + hk * 64:1280 + (hk + 1) * 64].rearrange("(kt p) n -> p kt n", p=128), writes=[])
        wr.writers = [wc.last]
        for (lo, w) in tok_tiles(0, NTOK):
            rd = [actA_r[b] for b in range(lo // 128, (lo + w) // 128)] + [wr]
            for j in range(2):
                ps, psr, _ = psA.next()
                P.mm_group(ps[:, 0:w], [(wt[:, kt, j * 128:(j + 1) * 128], actA[:, kt, lo:lo + w]) for kt in range(8)], psr, rd)
                P.evac(qB[:, j, lo:lo + w], ps[:, 0:w], [psr], [qB_r])
            ps, psr, _ = psA.next()
            P.mm_group(ps[:, 0:w], [(wt[:, kt, 256:384], actA[:, kt, lo:lo + w]) for kt in range(8)], psr, rd)
            P.evac(kB[:, lo:lo + w], ps[:, 0:w], [psr], [kB_r])
        for bi in range(18):
            ps, psr, _ = psA.next()
            P.mm_group(ps[:, 0:128], [(actA[:, kt, bi * 128:(bi + 1) * 128], wt[:, kt, 384:512]) for kt in range(8)], psr, [actA_r[bi], wr])
            P.evac(vB[:, bi, :], ps[:, 0:128], [psr], [vB_r])
        for i4 in range(4):
            qh = 4 * hk + i4
            wd_, wdr, wdc = wqd.next()
            for j in range(2):
                P.dma("pool", wdc, wd_[:, :, 64 * j:64 * j + 64],
                      w_b_in[:, qh * 64:(qh + 1) * 64].rearrange("(kt p) n -> p kt n", p=128), writes=[wdr] if j == 0 else [])
            wdr.writers = [wdc.last]
            ps, psr, _ = psA.next()
            P.mm_group(ps[:, 0:128], [(wd_[:, kt, :], actA[:, kt, NOWN:NTOK]) for kt in range(8)], psr, [actA_r[17], wdr])
            P.evac(qsd[:, qh, :], ps[:, 0:128], [psr], [qsd_r])
        for j in range(2):
            for a in range(2):
                qh = hk * 4 + 2 * j + a
                pl = slice(64 * a, 64 * a + 64)
                sl = float(-B_SLOPES[qh])
                P.op("dve", lambda e, sl=sl: e.scalar_tensor_tensor(out=tabB[:], in0=tab2[:, 0:256], scalar=sl, in1=tab2[:, 256:512],
                                                                     op0=ALU.mult, op1=ALU.add), reads=[Rc], writes=[tabB_r])
                P.op("dve", lambda e, sl=sl: e.scalar_tensor_tensor(out=tabnB[:], in0=tabn[:, 0:128], scalar=sl, in1=tabn[:, 128:256],
                                                                     op0=ALU.mult, op1=ALU.add), reads=[Rc], writes=[tabnB_r])
                chunks = [dict(c=c, q0=128 * c) for c in range(1, 17)]

                def stage1(ch, pl=pl, j=j):
                    q0 = ch["q0"]
                    S, Sr = psS.next()[:2]
                    ch["S"], ch["Sr"] = S, Sr
                    P.op("pe", lambda e: e.matmul(S[:, 0:128], lhsT=kB[pl, q0 - 128:q0], rhs=qB[pl, j, q0:q0 + 128], start=True, stop=True),
                         reads=[kB_r, qB_r], writes=[Sr])
                    b_ = P.op("pe", lambda e: e.matmul(S[:, 128:256], lhsT=kB[pl, q0:q0 + 128], rhs=qB[pl, j, q0:q0 + 128], start=True, stop=True),
                              reads=[kB_r, qB_r], writes=[])
                    Sr.writers = [b_]

                def stage2(ch):
                    S, Sr, c = ch["S"], ch["Sr"], ch["c"]
                    sc, scr = scS.next()[:2]
                    pp, ppr = pP.next()[:2]
                    ch["pp"], ch["ppr"] = pp, ppr
                    P.op("dve", lambda e: e.scalar_tensor_tensor(out=sc[:], in0=S[:, 0:256], scalar=scale_b, in1=tabB[:],
                                                                  op0=ALU.mult, op1=ALU.add), reads=[Sr, tabB_r], writes=[scr])
                    if c == 1:
                        P.op("act", lambda e: e.activation(out=pp[:, 0:128], in_=sc[:, 0:128], func=AF.Exp, bias=hmask[:, 0:1]),
                             reads=[scr, Rc], writes=[ppr])
                        P.op("act", lambda e: e.activation(out=pp[:, 128:256], in_=sc[:, 128:256], func=AF.Exp),
                             reads=[scr, ppr], writes=[ppr])
                    else:
                        P.op("act", lambda e: e.activation(out=pp[:], in_=sc[:], func=AF.Exp), reads=[scr], writes=[ppr])

                def stage3(ch, pl=pl, j=j, hk=hk, qh=qh):
                    pp, ppr, c, q0 = ch["pp"], ch["ppr"], ch["c"], ch["q0"]
                    O, Or = next_psO()
                    P.mm_group(O[:, 0, :], [(vB[:, c - 1, :], pp[:, 0:128]), (vB[:, c, :], pp[:, 128:256])], Or, [vB_r, ppr])
                    P.mm_group(O[:, 1, :], [(onesb[:, :], pp[:, 0:128]), (onesb[:, :], pp[:, 128:256])], P.res("dummy"), [ppr, Rc])
                    Or.writers = [P.ops["pe"][-1]]
                    dn, dnr = den.next()[:2]
                    P.op("dve", lambda e: e.tensor_scalar(out=dn[pl, :], in0=O[pl, 1, :], scalar1=esink[pl, qh:qh + 1], scalar2=None,
                                                           op0=ALU.add), reads=[Or, Rc], writes=[dnr])
                    P.op("dve", lambda e: e.reciprocal(out=dn[pl, :], in_=dn[pl, :]), reads=[dnr], writes=[dnr])
                    P.op("dve", lambda e: e.tensor_tensor(out=actB[pl, 2 * hk + j, q0:q0 + 128], in0=O[pl, 0, :], in1=dn[pl, :], op=ALU.mult),
                         reads=[Or, dnr], writes=[actB_r[c]])

                for i in range(len(chunks) + LOOK):
                    if i < len(chunks):
                        stage1(chunks[i])
                        stage2(chunks[i])
                    if i >= LOOK:
                        stage3(chunks[i - LOOK])
                S, Sr, _ = psS.next()
                P.mm_group(S[:, 0:128], [(kB[pl, NOWN:NTOK], qB[pl, j, NOWN:NTOK])], Sr, [kB_r, qB_r])
                sc, scr = scS.next()[:2]
                pp, ppr = pP.next()[:2]
                P.op("dve", lambda e, sc=sc, S=S: e.scalar_tensor_tensor(
                    out=sc[:, 0:128], in0=S[:, 0:128], scalar=scale_b, in1=tabnB[:], op0=ALU.mult, op1=ALU.add),
                    reads=[Sr, tabnB_r], writes=[scr])
                P.op("act", lambda e, sc=sc, pp=pp: e.activation(out=pp[:, 0:128], in_=sc[:, 0:128], func=AF.Exp), reads=[scr], writes=[ppr])
                O, Or = next_psO()
                P.mm_group(O[:, 0, :], [(vB[:, 17, :], pp[:, 0:128])], Or, [vB_r, ppr])
                P.mm_group(O[:, 1, :], [(onesb[:, :], pp[:, 0:128])], P.res("dummy"), [ppr, Rc])
                Or.writers = [P.ops["pe"][-1]]
                aBs = [accB_rr[(b_, a)] for b_ in range(16)]
                P.op("dve", lambda e, O=O, pl=pl, hk=hk, j=j: e.tensor_copy(out=accB[pl, :, 2 * hk + j, :], in_=O[pl, :, :]),
                     reads=[Or] + aBs, writes=aBs)

    sBi.close()
    ck(9)
    sBc = Scope()
    kcb = Ring(P, sBc.sb, "kcb", 3, [128, 512], BF16)
    KTb = Ring(P, sBc.sb, "KTb", 2, [128, 4, 128], BF16, chans=False)
    vdup = Ring(P, sBc.sb, "vdup", 2, [128, 4, 128], BF16, chans=False)
    scS2 = Ring(P, sBc.sb, "scS2b", 2, [128, 128], F32, chans=False)
    pP2 = Ring(P, sBc.sb, "pP2b", 2, [128, 128], BF16, chans=False)
    for b in range(16):
        kt_, kr, kch = kcb.next()
        P.dma("pool", kch, kt_[:], cbk[b, :, :], writes=[kr])
        pt, ptr, _ = psT.next()
        for m in range(2):
            P.op("pe", lambda e, m=m, pt=pt, kt_=kt_: e.transpose(out=pt[:, m * 128:(m + 1) * 128], in_=kt_[:, m * 128:(m + 1) * 128],
                                                                 identity=identb[:]),
                 reads=[kr, Rc], writes=[ptr] if m == 0 else [])
        ptr.writers = [P.ops["pe"][-1]]
        KTt, KTr = KTb.next()[:2]
        P.evac(KTt[:, 0:2, :].rearrange("p a b -> p (a b)"), pt[:, 0:256], [ptr], [KTr])
        vd, vdr = vdup.next()[:2]
        P.op("dve", lambda e, vd=vd, kt_=kt_: e.tensor_copy(out=vd[:, :, 0:64], in_=kt_[:, 256:512].rearrange("p (h d) -> p h d", d=64)),
             reads=[kr], writes=[vdr])
        P.op("act", lambda e, vd=vd, kt_=kt_: e.activation(out=vd[:, :, 64:128], in_=kt_[:, 256:512].rearrange("p (h d) -> p h d", d=64),
                                                           func=AF.Copy), reads=[kr, vdr], writes=[vdr])
        S, Sr, _ = psS.next()
        for qh in range(16):
            hk = qh // 4
            ka = hk % 2
            pl = slice(64 * ka, 64 * ka + 64)
            o_ = P.op("pe", lambda e, S=S, qh=qh, pl=pl, hk=hk, KTt=KTt, b=b: e.matmul(
                S[:, qh * 8:(qh + 1) * 8], lhsT=KTt[pl, hk // 2, :], rhs=qsd[pl, qh, b * 8:(b + 1) * 8], start=True, stop=True),
                reads=[KTr, qsd_r], writes=[Sr] if qh == 0 else [])
        Sr.writers = [o_]
        sc, scr = scS2.next()[:2]
        pp, ppr = pP2.next()[:2]
        P.op("dve", lambda e, sc=sc, S=S: e.scalar_tensor_tensor(out=sc[:], in0=S[:, 0:128], scalar=scale_b, in1=tsb[:],
                                                                  op0=ALU.mult, op1=ALU.add), reads=[Sr, Rc], writes=[scr])
        P.op("act", lambda e, sc=sc, pp=pp: e.activation(out=pp[:], in_=sc[:], func=AF.Exp), reads=[scr], writes=[ppr])
        O, Or = next_psO()
        Of = O.rearrange("p a b -> p (a b)")
        for qh in range(16):
            o_ = P.op("pe", lambda e, qh=qh, Of=Of, vd=vd, pp=pp: e.matmul(
                Of[:, qh * 8:(qh + 1) * 8], lhsT=vd[:, qh // 4, :], rhs=pp[:, qh * 8:(qh + 1) * 8], start=True, stop=True),
                reads=[vdr, ppr], writes=[Or] if qh == 0 else [])
        o_ = P.op("pe", lambda e, Of=Of, pp=pp: e.matmul(Of[:, 128:256], lhsT=onesb[:, :], rhs=pp[:, :], start=True, stop=True),
                  reads=[ppr, Rc], writes=[])
        Or.writers = [o_]
        for a in range(2):
            pl = slice(64 * a, 64 * a + 64)
            dst = accB[pl, :, :, b * 8:(b + 1) * 8]
            src = O[pl, :, :].rearrange("p a (t two q) -> p a t two q", two=2, q=8)[:, :, :, a, :]
            P.op("dve", lambda e, dst=dst, src=src: e.tensor_tensor(out=dst, in0=dst, in1=src, op=ALU.add),
                 reads=[Or, accB_rr[(b, a)]], writes=[accB_rr[(b, a)]])
    for t in range(8):
        for a in range(2):
            pl = slice(64 * a, 64 * a + 64)
            qh = 2 * t + a
            P.op("dve", lambda e, pl=pl, t=t, qh=qh: e.tensor_scalar(out=accB[pl, 1, t, :], in0=accB[pl, 1, t, :],
                                                                     scalar1=esink[pl, qh:qh + 1], scalar2=None, op0=ALU.add),
                 reads=accB_all + [Rc], writes=accB_all)
    P.op("dve", lambda e: e.reciprocal(out=accB[:, 1], in_=accB[:, 1]), reads=accB_all, writes=accB_all)
    P.op("dve", lambda e: e.tensor_tensor(out=actB[:, :, NOWN:NTOK], in0=accB[:, 0], in1=accB[:, 1], op=ALU.mult),
         reads=accB_all, writes=[actB_r[17]] + accB_all)

    sBc.close()
    sBs.close()
    ck(10)
    def out_proj_B():
        sc_ = Scope()
        wo = sc_.sb("woB", [128, 8, D], BF16)
        wor = P.res("woB")
        woc = P.chan("woB")
        P.dma("pool", woc, wo[:], w_b_out.rearrange("(kt p) n -> p kt n", p=128), writes=[wor])
        mods = mod_tiles(sc_, 1, [(2, "B")], None)
        scrF = Ring(P, sc_.sb, "scrF", 2, [128, D], F32, chans=False)
        for bi in range(1, 18):
            s = set_of(bi == 17)
            G, Gr = mods[(s, 2)]
            tf, tr = scrF.next()[:2]
            for half in range(2):
                ps, psr, _ = psA.next()
                P.mm_group(ps[:], [(actB[:, kt, bi * 128:(bi + 1) * 128], wo[:, kt, half * 512:(half + 1) * 512]) for kt in range(8)],
                           psr, [actB_r[bi], wor])
                o_ = P.op("dve", lambda e, tf=tf, ps=ps, G=G, half=half: e.tensor_tensor(
                    out=tf[:, half * 512:(half + 1) * 512], in0=ps[:], in1=G[:, half * 512:(half + 1) * 512], op=ALU.mult),
                    reads=[psr, Gr], writes=[tr] if half == 0 else [])
            tr.writers = [o_]
            P.op("pool", lambda e, bi=bi, tf=tf: e.tensor_tensor(out=xr[:, bi, :], in0=xr[:, bi, :], in1=tf[:], op=ALU.add),
                 reads=[tr, xr_r[bi]], writes=[xr_r[bi]])
        sc_.close()

    out_proj_B()
    sB.close()
    ffn(1, list(range(1, 18)))

    fs = Scope()
    gf = fs.sb("gfinal", [128, D], F32)
    gfr = P.res("gfinal")
    gfc = P.chan("gfinal")
    P.dma("sp", gfc, gf[:], norm_final_g.partition_broadcast(128), writes=[gfr])
    stat = Ring(P, fs.sb, "stat", 4, [128, 2], F32, chans=False)
    scrB = Ring(P, fs.sb, "scrB", 2, [128, D], BF16, chans=False)
    ystg = Ring(P, fs.sb, "ystg", 3, [128, D], F32)
    for c_ in ystg.c:
        out_chans.append(c_)
    for bi in range(1, 18):
        st, sr = stat.next()[:2]
        jb, jr = scrB.next()[:2]
        P.op("act", lambda e, jb=jb, st=st, bi=bi: e.activation(out=jb[:], in_=xr[:, bi, :], func=AF.Square, scale=1.0 / 32,
                                                                accum_out=st[:, 0:1]), reads=[xr_r[bi]], writes=[jr, sr])
        P.op("act", lambda e, st=st: e.activation(out=st[:, 1:2], in_=st[:, 0:1], func=AF.Sqrt, bias=epst[:], scale=1.0),
             reads=[sr, Rc], writes=[sr])
        P.op("dve", lambda e, st=st: e.reciprocal(out=st[:, 1:2], in_=st[:, 1:2]), reads=[sr], writes=[sr])
        yt, yr, yc = ystg.next()
        P.op("dve", lambda e, yt=yt, st=st, bi=bi: e.scalar_tensor_tensor(out=yt[:], in0=xr[:, bi, :], scalar=st[:, 1:2], in1=gf[:],
                                                                          op0=ALU.mult, op1=ALU.mult),
             reads=[xr_r[bi], sr, gfr], writes=[yr])
        dst = y_p[(bi - 1) * 128:bi * 128, :] if bi < 17 else y_s
        P.dma("sp", yc, dst, yt[:], reads=[yr])
    fs.close()

    P.emit(final_chans=out_chans)
    g.st.close()
    return nc


def _const_tables():
    kk = np.arange(128)[:, None]
    cc = np.arange(256)[None, :]
    dist2 = np.where(cc < 128, 128 + cc - kk, (cc - 128) - kk).astype(np.float64)
    valid2 = (dist2 >= 0) & (dist2 <= 128)
    tab2 = np.concatenate([np.where(valid2, dist2, 0.0), np.where(valid2, 0.0, NEG)], axis=1).astype(np.float32)
    bk, tk = kk // 8, kk % 8
    cq = np.arange(128)[None, :]
    bq, iq = cq // 8, cq % 8
    tn = []
    for (W, d) in A_GROUPS:
        dd = (iq - tk).astype(np.float64)
        ok = (bk == bq) & (dd >= 0) & ((iq - tk) % d == 0)
        tn += [np.where(ok, dd, 0.0), np.where(ok, 0.0, NEG)]
    tabn = np.concatenate(tn, axis=1).astype(np.float32)
    m = np.arange(128)[:, None, None]
    sl = A_SLOPES
    i8 = np.arange(8)[None, None, :]
    t0 = np.where(m >= i8, -sl[0][None, :, None] * (128 + i8 - m), NEG)
    q2 = np.arange(2)[None, None, :]
    t1 = np.where(m >= q2, -sl[1][None, :, None] * 4.0 * (128 + q2 - m), NEG)
    t2 = -sl[2][None, :, None] * 16.0 * (128 - m) + 0.0 * np.arange(1)[None, None, :]
    tsa = np.concatenate([t0.reshape(128, 64), t1.reshape(128, 16), t2.reshape(128, 8)], axis=1).astype(np.float32)
    tsb = np.where(m >= i8, -B_SLOPES[None, :, None] * (128 + i8 - m), NEG).reshape(128, 128).astype(np.float32)
    return tab2, tabn, tsa, tsb


_NC_CACHE = {}
_CORES = list(range(8))


def kernel(**inputs):
    f = lambda k: np.ascontiguousarray(np.asarray(inputs[k], dtype=np.float32))
    x_prompt, x_sample = f("x_prompt"), f("x_sample")
    c_prompt, c_sample = f("c_prompt"), f("c_sample")
    cache_a_w128, cache_a_w512, cache_a_w2048, cache_b = f("cache_a_w128"), f("cache_a_w512"), f("cache_a_w2048"), f("cache_b")
    tab2, tabn, tsa, tsb = _const_tables()
    shared = {
        "ident": np.eye(128, dtype=np.float32), "tab2": tab2, "tabn": tabn, "tsa": tsa, "tsb": tsb,
        "norm_mix_g": f("norm_mix_g"), "norm_ffn_g": f("norm_ffn_g"), "w_ada": f("w_ada"), "b_ada": f("b_ada"),
        "w_a_in": f("w_a_in")[0], "w_a_out": f("w_a_out")[0], "w_b_in": f("w_b_in")[0],
        "b_sinks": f("b_sinks").reshape(16), "w_b_out": f("w_b_out")[0],
        "w_ffn_gate": f("w_ffn_gate"), "w_ffn_up": f("w_ffn_up"), "w_ffn_down": f("w_ffn_down"),
        "norm_final_g": f("norm_final_g"),
    }
    in_maps = []
    for c in _CORES:
        b, ci = c // 4, c % 4
        T0 = ci * 2048
        lo = T0 - 2176
        xp = np.zeros((NEXT, D), np.float32)
        s_lo = max(lo, 0)
        xp[s_lo - lo:] = x_prompt[b, s_lo:T0 + 2048]
        hm = np.zeros((128, 4), np.float32)
        if ci == 0:
            hm[:, 0] = NEG
            hm[:32, 1] = NEG
            hm[:8, 2] = NEG
        sbs = slice(16 * c, 16 * c + 16)
        m = dict(shared)
        m.update({
            "xp": xp, "xs": np.ascontiguousarray(x_sample[sbs].reshape(128, D)),
            "cvec": np.ascontiguousarray(np.concatenate([c_prompt[b:b + 1], c_sample[sbs]], axis=0)),
            "ca128": np.ascontiguousarray(cache_a_w128[0, sbs].reshape(16, 128, 2048)),
            "ca512": np.ascontiguousarray(cache_a_w512[0, sbs].reshape(16, 512, 2048)),
            "ca2048": np.ascontiguousarray(cache_a_w2048[0, sbs].reshape(16, 2048, 2048)),
            "cbk": np.ascontiguousarray(cache_b[0, sbs].reshape(16, 128, 512)),
            "hmask": hm,
        })
        in_maps.append(m)
    if "nc" not in _NC_CACHE:
        _NC_CACHE["nc"] = build_program()
    if _NC_CACHE.get("maps_only"):
        return in_maps
    res = run_bass_kernel_spmd(_NC_CACHE["nc"], in_maps, core_ids=list(range(8)))
    R = res.results
    y_prompt = np.zeros((2, 8192, D), np.float32)
    for c in range(8):
        y_prompt[c // 4, (c % 4) * 2048:(c % 4 + 1) * 2048] = R[c]["y_p"]
    y_sample = np.concatenate([R[c]["y_s"].reshape(16, 8, D) for c in range(8)], axis=0)
    pa128 = np.stack([R[3]["pa128"], R[7]["pa128"]])[None]
    pa512 = np.stack([R[3]["pa512"], R[7]["pa512"]])[None]
    pa2048 = np.stack([R[3]["pa2048"], R[7]["pa2048"]])[None]
    pbo = np.stack([R[3]["pb"].reshape(128, 2, 4, 64), R[7]["pb"].reshape(128, 2, 4, 64)])[None]
    sa_ = [np.concatenate([R[c][n].reshape(16, 8, 2, 8, 128) for c in range(8)], axis=0)[None] for n in ("sa128", "sa512", "sa2048")]
    sbo = np.concatenate([R[c]["sb"].reshape(16, 8, 2, 4, 64) for c in range(8)], axis=0)[None]
    return (y_prompt, y_sample, np.ascontiguousarray(pa128), np.ascontiguousarray(pa512), np.ascontiguousarray(pa2048),
            np.ascontiguousarray(pbo), sa_[0], sa_[1], sa_[2], sbo)
```
